# Optimizing a Trainium2 kernel written in Bass

```python
import jax, jax.numpy as jnp
from jax import lax
import numpy as np

D_MODEL = 2048
BATCH = 2
SEQ = 8192
DEPTH = 1

CHUNK = 64
D_MIX = D_MODEL
D_MLSTM = D_MIX // 2
MLSTM_HEADS = 4
MLSTM_HEAD_DIM = D_MLSTM // MLSTM_HEADS
D_RGLRU = D_MIX - D_MLSTM
RG_BLOCKS = 8
RG_BLOCK_DIM = D_RGLRU // RG_BLOCKS
CONV_WIDTH = 4
RG_C = 8.0
D_FF = 5632
FFN_WEIGHT = 0.5
ALPHA = float((2 * DEPTH) ** 0.25)
BETA = float((8 * DEPTH) ** -0.25)
LN_EPS = 1e-5
N_SUBLAYERS = 3
IN_COLS = 4 * D_MLSTM + 2 * MLSTM_HEADS + 2 * D_RGLRU

kernel_name = "hybrid_mlstm_rglru_macaron_block"


def _layer_norm(x, gain=None, bias=None):
    xf = x.astype(jnp.float32)
    mu = xf.mean(-1, keepdims=True)
    var = jnp.square(xf - mu).mean(-1, keepdims=True)
    y = (xf - mu) * lax.rsqrt(var + LN_EPS)
    if gain is not None:
        y = y * gain.astype(jnp.float32) + bias.astype(jnp.float32)
    return y.astype(x.dtype)


def _swiglu(u, w13, w2):
    g, v = jnp.split(u @ w13, 2, axis=-1)
    return (jax.nn.silu(g) * v) @ w2


def _mlstm(q, k, v, i_pre, f_pre):
    B, S, H, Dh = q.shape
    nc = S // CHUNK

    def to_chunks(t):
        t = t.astype(jnp.float32).reshape((B, nc, CHUNK, H) + t.shape[3:])
        return jnp.moveaxis(jnp.moveaxis(t, 1, 0), 3, 2)

    qc = to_chunks(q)
    kc = to_chunks(k) * (Dh ** -0.5)
    vc = to_chunks(v)
    li = to_chunks(i_pre)
    lf = to_chunks(jax.nn.log_sigmoid(f_pre.astype(jnp.float32)))
    tri = jnp.tril(jnp.ones((CHUNK, CHUNK), dtype=bool))

    def step(carry, xs):
        C, n, m = carry
        qb, kb, vb, lib, lfb = xs
        b = jnp.cumsum(lfb, axis=-1)
        d = b[..., :, None] - b[..., None, :] + lib[..., None, :]
        d = jnp.where(tri, d, -jnp.inf)
        inter = b + m[..., None]
        m_t = jnp.maximum(inter, d.max(-1))
        w_intra = jnp.exp(d - m_t[..., None])
        w_inter = jnp.exp(inter - m_t)
        s = jnp.einsum('bhtd,bhsd->bhts', qb, kb) * w_intra
        num = (w_inter[..., None] * jnp.einsum('bhtd,bhde->bhte', qb, C)
               + jnp.einsum('bhts,bhse->bhte', s, vb))
        den = w_inter * jnp.einsum('bhtd,bhd->bht', qb, n) + s.sum(-1)
        h = num / jnp.maximum(jnp.abs(den), jnp.exp(-m_t))[..., None]
        b_last = b[..., -1]
        w_log = b_last[..., None] - b + lib
        m_new = jnp.maximum(b_last + m, w_log.max(-1))
        decay = jnp.exp(b_last + m - m_new)
        w_s = jnp.exp(w_log - m_new[..., None])
        C = decay[..., None, None] * C + jnp.einsum('bhs,bhsd,bhse->bhde', w_s, kb, vb)
        n = decay[..., None] * n + jnp.einsum('bhs,bhsd->bhd', w_s, kb)
        return (C, n, m_new), h

    init = (jnp.zeros((B, H, Dh, Dh), jnp.float32),
            jnp.zeros((B, H, Dh), jnp.float32),
            jnp.zeros((B, H), jnp.float32))
    _, hs = lax.scan(step, init, (qc, kc, vc, li, lf))
    hs = jnp.transpose(hs, (1, 0, 3, 2, 4)).reshape(B, S, H, Dh)
    return hs.astype(v.dtype)


def _causal_depthwise_conv(x, w, b):
    S = x.shape[1]
    xp = jnp.pad(x, ((0, 0), (CONV_WIDTH - 1, 0), (0, 0)))
    y = b
    for tap in range(CONV_WIDTH):
        y = y + xp[:, tap:tap + S] * w[tap]
    return y


def _rg_lru(x, wa, ba, wx, bx, lam):
    B, S, C = x.shape
    xb = x.reshape(B, S, RG_BLOCKS, RG_BLOCK_DIM)
    r = jax.nn.sigmoid(jnp.einsum('bsnd,nde->bsne', xb, wa).reshape(B, S, C) + ba)
    i = jax.nn.sigmoid(jnp.einsum('bsnd,nde->bsne', xb, wx).reshape(B, S, C) + bx)
    log_a = -RG_C * r.astype(jnp.float32) * jax.nn.softplus(-lam.astype(jnp.float32))
    a = jnp.exp(log_a)
    u = jnp.sqrt(-jnp.expm1(2.0 * log_a)) * (i * x).astype(jnp.float32)

    def combine(left, right):
        a1, b1 = left
        a2, b2 = right
        return a1 * a2, a2 * b1 + b2

    _, h = lax.associative_scan(combine, (a, u), axis=1)
    return h.astype(x.dtype)


def setup_inputs(seed: int = 0) -> dict:
    key = jax.random.key(seed)
    ks = jax.random.split(key, 24)
    nrm = jax.random.normal
    f32 = jnp.float32
    x = nrm(ks[0], (BATCH, SEQ, D_MODEL), f32)
    c = nrm(ks[1], (BATCH, D_MODEL), f32)
    w_ada = nrm(ks[2], (DEPTH, D_MODEL, N_SUBLAYERS * 3 * D_MODEL), f32) * D_MODEL ** -0.5
    b_ada = 0.01 * nrm(ks[3], (DEPTH, N_SUBLAYERS * 3 * D_MODEL), f32)
    ffn1_w13 = nrm(ks[4], (DEPTH, D_MODEL, 2 * D_FF), f32) * D_MODEL ** -0.5
    ffn1_w2 = nrm(ks[5], (DEPTH, D_FF, D_MODEL), f32) * (D_FF ** -0.5 * BETA)
    w_in = nrm(ks[6], (DEPTH, D_MODEL, IN_COLS), f32) * D_MODEL ** -0.5
    f_start = 4 * D_MLSTM + MLSTM_HEADS
    b_in = 0.01 * nrm(ks[7], (DEPTH, IN_COLS), f32)
    b_in = b_in.at[:, f_start:f_start + MLSTM_HEADS].add(
        jnp.linspace(3.0, 6.0, MLSTM_HEADS, dtype=f32))
    mlstm_norm_g = 1.0 + 0.02 * nrm(ks[8], (DEPTH, MLSTM_HEADS, MLSTM_HEAD_DIM), f32)
    rg_conv_w = nrm(ks[9], (DEPTH, CONV_WIDTH, D_RGLRU), f32) * CONV_WIDTH ** -0.5
    rg_conv_b = 0.01 * nrm(ks[10], (DEPTH, D_RGLRU), f32)
    rg_wa = nrm(ks[11], (DEPTH, RG_BLOCKS, RG_BLOCK_DIM, RG_BLOCK_DIM), f32) * RG_BLOCK_DIM ** -0.5
    rg_ba = 0.01 * nrm(ks[12], (DEPTH, D_RGLRU), f32)
    rg_wx = nrm(ks[13], (DEPTH, RG_BLOCKS, RG_BLOCK_DIM, RG_BLOCK_DIM), f32) * RG_BLOCK_DIM ** -0.5
    rg_bx = 0.01 * nrm(ks[14], (DEPTH, D_RGLRU), f32)
    a_pow = jax.random.uniform(ks[15], (DEPTH, D_RGLRU), f32, 0.9, 0.999)
    p = a_pow ** (1.0 / RG_C)
    rg_lambda = jnp.log(p) - jnp.log1p(-p)
    w_out = nrm(ks[16], (DEPTH, D_MIX, D_MODEL), f32) * (D_MIX ** -0.5 * BETA)
    ffn2_w13 = nrm(ks[17], (DEPTH, D_MODEL, 2 * D_FF), f32) * D_MODEL ** -0.5
    ffn2_w2 = nrm(ks[18], (DEPTH, D_FF, D_MODEL), f32) * (D_FF ** -0.5 * BETA)
    ln_g = 1.0 + 0.02 * nrm(ks[19], (DEPTH, N_SUBLAYERS, D_MODEL), f32)
    ln_b = 0.01 * nrm(ks[20], (DEPTH, N_SUBLAYERS, D_MODEL), f32)
    return {"x": x, "c": c, "w_ada": w_ada, "b_ada": b_ada,
            "ffn1_w13": ffn1_w13, "ffn1_w2": ffn1_w2,
            "w_in": w_in, "b_in": b_in, "mlstm_norm_g": mlstm_norm_g,
            "rg_conv_w": rg_conv_w, "rg_conv_b": rg_conv_b,
            "rg_wa": rg_wa, "rg_ba": rg_ba, "rg_wx": rg_wx, "rg_bx": rg_bx,
            "rg_lambda": rg_lambda, "w_out": w_out,
            "ffn2_w13": ffn2_w13, "ffn2_w2": ffn2_w2,
            "ln_g": ln_g, "ln_b": ln_b}


def reference(x, c, w_ada, b_ada, ffn1_w13, ffn1_w2, w_in, b_in, mlstm_norm_g,
              rg_conv_w, rg_conv_b, rg_wa, rg_ba, rg_wx, rg_bx, rg_lambda, w_out,
              ffn2_w13, ffn2_w2, ln_g, ln_b):
    B, S, _ = x.shape
    H, Dh = MLSTM_HEADS, MLSTM_HEAD_DIM
    for l in range(DEPTH):
        mod = (jax.nn.silu(c) @ w_ada[l] + b_ada[l]).reshape(B, N_SUBLAYERS, 3, D_MODEL)
        shift, scale, gate = mod[:, :, 0], mod[:, :, 1], mod[:, :, 2]

        def modulate(h, s):
            return _layer_norm(h) * (1.0 + scale[:, s, None, :]) + shift[:, s, None, :]

        y = _swiglu(modulate(x, 0), ffn1_w13[l], ffn1_w2[l])
        x = _layer_norm(ALPHA * x + FFN_WEIGHT * gate[:, 0, None, :] * y, ln_g[l, 0], ln_b[l, 0])

        u = modulate(x, 1)
        proj = u @ w_in[l] + b_in[l]
        q, k, v, o, i_pre, f_pre, x_rg, g_rg = jnp.split(
            proj, [D_MLSTM, 2 * D_MLSTM, 3 * D_MLSTM, 4 * D_MLSTM,
                   4 * D_MLSTM + H, 4 * D_MLSTM + 2 * H,
                   4 * D_MLSTM + 2 * H + D_RGLRU], axis=-1)
        h_m = _mlstm(q.reshape(B, S, H, Dh), k.reshape(B, S, H, Dh),
                     v.reshape(B, S, H, Dh), i_pre, f_pre)
        h_m = (_layer_norm(h_m) * mlstm_norm_g[l]).reshape(B, S, D_MLSTM) * jax.nn.sigmoid(o)
        x_rg = _causal_depthwise_conv(x_rg, rg_conv_w[l], rg_conv_b[l])
        h_r = _rg_lru(x_rg, rg_wa[l], rg_ba[l], rg_wx[l], rg_bx[l], rg_lambda[l])
        h_r = h_r * jax.nn.gelu(g_rg)
        y = jnp.concatenate([h_m, h_r], axis=-1) @ w_out[l]
        x = _layer_norm(ALPHA * x + gate[:, 1, None, :] * y, ln_g[l, 1], ln_b[l, 1])

        y = _swiglu(modulate(x, 2), ffn2_w13[l], ffn2_w2[l])
        x = _layer_norm(ALPHA * x + FFN_WEIGHT * gate[:, 2, None, :] * y, ln_g[l, 2], ln_b[l, 2])
    return x
```

```python
import numpy as np
import os
BIS = int(os.environ.get('BIS', '0'))
import concourse.bass as bass
import concourse.mybir as mybir
from concourse.bass_utils import run_bass_kernel_spmd
from contextlib import ExitStack

F32 = mybir.dt.float32
BF16 = mybir.dt.bfloat16
AF = mybir.ActivationFunctionType
ALU = mybir.AluOpType
AX = mybir.AxisListType

ENGS = ["pe", "act", "dve", "pool", "sp"]

D = 2048
DFF = 5632
NCORE = 8
GRP = 4
DH = 256
NH = 4
DM = 1024
DR = 1024
INC = 6152
ALPHA = float(2 ** 0.25)
FFW = 0.5
EPS = 1e-5
EPSZ = EPS / (ALPHA * ALPHA)
KSC = float(DH ** -0.5)
ADA_W = 18432 // GRP
NEG = -30000.0
NSLOT = 3


class Buf:
    __slots__ = ("name", "w", "r", "rd")

    def __init__(self, name=""):
        self.name = name
        self.w = None
        self.r = {}
        self.rd = []


class Op:
    __slots__ = ("eng", "fn", "waits", "inc", "idx", "dma", "dsem", "dval", "val", "ccinc")


class Prog:
    def __init__(self, nc, n_dma_sems=(("sp", 28), ("pool", 28), ("act", 8))):
        self.nc = nc
        self.ops = {e: [] for e in ENGS}
        self.known = {e: {} for e in ENGS}
        self.ndma = {e: 0 for e in ENGS}
        self.kdma = dict(n_dma_sems)
        self.out_dma = []
        self.last = {e: None for e in ENGS}
        self.ncc = {}

    def _dep(self, op, d):
        if d is op:
            return
        if d.dma:
            key = ("dma", d.eng, d.dsem if not isinstance(d.dsem, tuple) else d.dsem[0])
            if self.known[op.eng].get(key, 0) >= d.dval:
                return
            self.known[op.eng][key] = d.dval
            op.waits.append(d)
            return
        key = d.eng
        if self.known[op.eng].get(key, -1) >= d.idx:
            return
        self.known[op.eng][key] = d.idx
        d.inc = True
        op.waits.append(d)

    def emit(self, eng, fn, reads=(), writes=(), dma=False):
        op = Op()
        op.eng = eng
        op.fn = fn
        op.waits = []
        op.inc = False
        op.dma = dma
        op.idx = len(self.ops[eng])
        op.val = None
        op.dsem = None
        op.dval = None
        op.ccinc = None
        if dma:
            k = self.kdma[eng]
            i = self.ndma[eng]
            self.ndma[eng] += 1
            op.dsem = i % k
            op.dval = 16 * (i // k + 1)
            self.out_dma.append(op)
        cand = {}

        def add(d, raw):
            if d is None or d is op:
                return
            if d.dma:
                key = ("dma", d.eng, d.dsem if not isinstance(d.dsem, tuple) else d.dsem[0])
                if key not in cand or cand[key].dval < d.dval:
                    cand[key] = d
                return
            if d.eng == eng and not dma:
                if eng == "pe" or not raw:
                    return
            key = d.eng
            if key not in cand or cand[key].idx < d.idx:
                cand[key] = d

        for b in reads:
            add(b.w, True)
        for b in writes:
            add(b.w, False)
            for r in b.r.values():
                add(r, False)
            for r in b.rd:
                add(r, False)
        for d in cand.values():
            self._dep(op, d)
        for b in reads:
            if dma:
                b.rd.append(op)
            else:
                b.r[eng] = op
        for b in writes:
            b.w = op
            b.r = {}
            b.rd = []
        self.ops[eng].append(op)
        if not dma:
            self.last[eng] = op
        return op

    def barrier(self):
        lasts = dict(self.last)
        dmas = list(self.out_dma)
        self.out_dma = []
        for e in ENGS:
            op = self.emit(e, lambda en: en.nop())
            for f, d in lasts.items():
                if f != e and f != "sp" and d is not None:
                    self._dep(op, d)
            for d in dmas:
                self._dep(op, d)

    def cc(self, fn, sem, name, reads=(), writes=()):
        op = self.emit("pool", fn, reads, writes, dma=True)
        self.ndma["pool"] -= 1
        self.ncc[name] = self.ncc.get(name, 0) + 1
        op.dsem = (name, sem)
        op.dval = self.ncc[name]
        op.ccinc = 1
        return op

    def dma(self, eng, out, in_, reads=(), writes=(), **kw):
        return self.emit(eng, lambda e: e.dma_start(out=out, in_=in_, **kw), reads, writes, dma=True)

    def E(self, eng, method, reads=(), writes=(), **kw):
        return self.emit(eng, lambda e: getattr(e, method)(**kw), reads, writes)

    def run(self, es):
        nc = self.nc
        esem = {e: es.enter_context(nc.semaphore("es_" + e)) for e in ["pe", "act", "dve", "pool"]}
        dsem = {}
        for e, k in self.kdma.items():
            if self.ndma[e] > 0:
                dsem[e] = [es.enter_context(nc.semaphore(f"ds_{e}_{i}")) for i in range(min(k, self.ndma[e]))]
        for e in ENGS:
            c = 0
            for op in self.ops[e]:
                if (not op.dma) and op.inc:
                    c += 1
                op.val = c
        block = es.enter_context(nc.Block())
        handles = {"pe": "tensor", "act": "scalar", "dve": "vector", "pool": "gpsimd", "sp": "sync"}

        def make(ename):
            def body(eng):
                for op in self.ops[ename]:
                    for d in op.waits:
                        if d.dma and isinstance(d.dsem, tuple):
                            eng.wait_ge(d.dsem[1], d.dval)
                        elif d.dma:
                            eng.wait_ge(dsem[d.eng][d.dsem], d.dval)
                        else:
                            eng.wait_ge(esem[d.eng], d.val)
                    if op.dma and isinstance(op.dsem, tuple):
                        op.fn(eng).then_inc(op.dsem[1])
                    elif op.dma:
                        if op.dval > 16:
                            eng.wait_ge(dsem[ename][op.dsem], op.dval - 16)
                        op.fn(eng).then_inc(dsem[ename][op.dsem], 16)
                    else:
                        inst = op.fn(eng)
                        if op.inc:
                            inst.then_inc(esem[ename], 1)
            return body

        for ename in ENGS:
            if self.ops[ename]:
                getattr(block, handles[ename])(make(ename))


class K:
    pass


def build(NT, MIX=True, STOP=9):
    NTILE = NT // 512
    NCH = NT // 128
    nc = bass.Bass("TRN2", target_bir_lowering=False)

    def din(name, shape, dt=F32):
        if (STOP <= 0 or BIS >= 10) and name in ("ffn1_w13", "ffn1_w2", "ffn2_w13", "ffn2_w2", "w_out"):
            shape = [128, 128]
        return nc.dram_tensor(name, list(shape), dt, kind="ExternalInput").ap()

    def dscr(name, shape, dt=F32):
        return nc.dram_tensor(name, list(shape), dt)

    x_d = din("x", [NT, D])
    cT_d = din("cT", [128, 16])
    wada_d = din("w_ada", [D, ADA_W])
    bada_d = din("b_ada", [1, ADA_W])
    w13_d = [din("ffn1_w13", [D, 2 * DFF]), din("ffn2_w13", [D, 2 * DFF])]
    w2_d = [din("ffn1_w2", [DFF, D]), din("ffn2_w2", [DFF, D])]
    win_d = din("w_in", [D, INC])
    wout_d = din("w_out", [D, D])
    bfm_d = din("b_fm", [128, 32])
    btm_d = din("b_tm", [1, 3072])
    bgi_d = din("b_gi", [4, 1])
    bgf_d = din("b_gf", [4, 1])
    ng_d = din("norm_g", [1, DM])
    rgp_d = din("rgp", [128, 64])
    rgwa_d = din("rg_wa", [8, 128, 128])
    rgwx_d = din("rg_wx", [8, 128, 128])
    lng_d = din("ln_g", [3, D])
    lnb_d = din("ln_b", [3, D])
    fl_d = din("flags", [128, 8])
    out_d = nc.dram_tensor("out", [NT, D], F32, kind="ExternalOutput").ap()

    x1_s = dscr("x1_s", [NT, D]).ap()
    x2_s = dscr("x2_s", [NT, D]).ap()
    qT_s = dscr("qT_s", [DM, NT], BF16).ap()
    kT_s = dscr("kT_s", [DM, NT], BF16).ap()
    ktm_s = dscr("ktm_s", [NT, DM], BF16).ap()
    vtm_s = dscr("vtm_s", [NT, DM], BF16).ap()
    og_s = dscr("og_s", [NT, DM], BF16).ap()
    xrg_s = dscr("xrg_s", [DR, NT]).ap()
    gg_s = dscr("gg_s", [DR, NT]).ap()
    gat_s = dscr("gat_s", [8, NT]).ap()
    hc_s = dscr("hc_s", [D, NT], BF16).ap()
    u3_s = dscr("u3_s", [D, NT], BF16).ap()
    modin_t = dscr("modin", [1, ADA_W])
    modout_t = dscr("modout", [GRP, ADA_W])
    G1W = 2112
    NG1 = 8
    G1C = G1W // NG1
    g1in_t = [dscr(f"g1in{k}", [128, G1C]) for k in range(NG1)]
    g1out_t = [dscr(f"g1out{k}", [GRP * 128, G1C]) for k in range(NG1)]
    g2in_t = dscr("g2in", [128, 16])
    g2out_t = dscr("g2out", [GRP * 128, 16])
    modout = modout_t.ap()
    groups = [[0, 1, 2, 3], [4, 5, 6, 7]]

    es = ExitStack()
    with es:
        P = Prog(nc)
        cc_sems = {n: es.enter_context(nc.semaphore("cc_" + n)) for n in ["mod", "g1", "g2"]}

        sbn = [0]

        def sb(stack, name, shape, dt=F32):
            sbn[0] += 1
            return stack.enter_context(nc.sbuf_tensor(f"sb{sbn[0]}_{name}", list(shape), dt))

        ring = [sb(es, f"ring{i}", [128, 8192], BF16) for i in range(NSLOT)]
        ringb = [Buf(f"ring{i}") for i in range(NSLOT)]
        ident = sb(es, "ident", [128, 128], F32)
        identb = sb(es, "identb", [128, 128], BF16)
        mask01 = sb(es, "mask01", [128, 128], F32)
        ones_bf = sb(es, "ones_bf", [128, 128], BF16)
        sel4 = sb(es, "sel4", [4, 4, 128], F32)
        modT = sb(es, "modT", [128, 144], F32)
        bfm = sb(es, "bfm", [128, 32], F32)
        bhl = sb(es, "bhl", [128, 3072], BF16)
        bgi = sb(es, "bgi", [4, 1], F32)
        bgf = sb(es, "bgf", [4, 1], F32)
        wg = sb(es, "wg", [128, 16, 8], BF16)
        rgp = sb(es, "rgp_sb", [128, 64], F32)
        fl = sb(es, "fl_sb", [128, 8], F32)
        small = sb(es, "small", [128, 64], F32)
        cst = Buf("const")
        mstb = Buf("mst")
        pstb = Buf("pst")
        modb = Buf("modT")
        psum = [es.enter_context(nc.psum_tensor(f"ps{i}", [128, 512], F32)) for i in range(8)]
        psb = [Buf(f"ps{i}") for i in range(8)]

        plan = []
        state = {"issued": 0, "next": 0}

        def ws_issue_upto(k):
            while state["issued"] <= min(k, len(plan) - 1):
                i = state["issued"]
                plan[i](i % NSLOT)
                state["issued"] += 1

        def ws_get(look=NSLOT - 1):
            i = state["next"]
            state["next"] += 1
            ws_issue_upto(i + look)
            return i % NSLOT

        def u_ada(u):
            def f(s):
                P.dma("sp", ring[s][:, 0:8192].bitcast(F32).rearrange("p (k c) -> p k c", k=16),
                      wada_d[:, u * 256:(u + 1) * 256].rearrange("(k p) c -> p k c", p=128), writes=[ringb[s]])
            return f

        def u_w13(l, u):
            def f(s):
                v = ring[s][:, :].rearrange("p (k c) -> p k c", k=16)
                P.dma("pool", v[:, :, 0:256], w13_d[l][:, u * 256:(u + 1) * 256].rearrange("(k p) c -> p k c", p=128),
                      writes=[ringb[s]])
                P.dma("pool", v[:, :, 256:512],
                      w13_d[l][:, DFF + u * 256:DFF + (u + 1) * 256].rearrange("(k p) c -> p k c", p=128),
                      writes=[ringb[s]])
            return f

        def u_rows(src, u):
            def f(s):
                v = ring[s][:, :].rearrange("p (j d) -> p j d", j=4)
                for hf in range(2):
                    P.dma("pool", v[:, :, hf * 1024:(hf + 1) * 1024],
                          src[u * 512:(u + 1) * 512, hf * 1024:(hf + 1) * 1024].rearrange("(j p) d -> p j d", p=128),
                          writes=[ringb[s]])
            return f

        WIN_C0 = [0, 512, 1024, 1536, 2048, 2560, 3072, 3584, 4104, 4616, 5128, 5640]

        def u_win(i):
            def f(s):
                v = ring[s][:, :].rearrange("p (k c) -> p k c", k=16)
                P.dma("pool", v, win_d[:, WIN_C0[i]:WIN_C0[i] + 512].rearrange("(k p) c -> p k c", p=128),
                      writes=[ringb[s]])
            return f

        for u in range(ADA_W // 256):
            plan.append(u_ada(u))
        for t in range(NTILE if (STOP > 0 and BIS < 10) else 0):
            for u in range(22):
                plan.append(u_w13(0, u))
            for u in range(11):
                plan.append(u_rows(w2_d[0], u))
            for i in range(12):
                plan.append(u_win(i))
        for t in range(NTILE if (STOP > 0 and BIS < 10) else 0):
            for u in range(22):
                plan.append(u_w13(1, u))
            for u in range(11):
                plan.append(u_rows(w2_d[1], u))

        P.E("pool", "memset", writes=[cst], ap=ident[:], constant=0.0)
        P.E("pool", "affine_select", reads=[cst], writes=[cst], out=ident[:], in_=ident[:], pattern=[[-1, 128]],
            compare_op=ALU.not_equal, fill=1.0, base=0, channel_multiplier=1)
        P.E("pool", "memset", writes=[cst], ap=mask01[:], constant=1.0)
        P.E("pool", "affine_select", reads=[cst], writes=[cst], out=mask01[:], in_=mask01[:], pattern=[[1, 128]],
            compare_op=ALU.is_ge, fill=0.0, base=0, channel_multiplier=-1)
        P.E("dve", "tensor_copy", reads=[cst], writes=[cst], out=identb[:], in_=ident[:])
        P.E("dve", "memset", writes=[cst], ap=ones_bf[:], constant=1.0)
        for h in range(4):
            P.E("dve", "tensor_copy", reads=[cst], writes=[cst], out=sel4[:, h, :],
                in_=ident[0:4, h:h + 1].to_broadcast([4, 128]))
        for dst, src in [(bfm, bfm_d), (bgi, bgi_d), (bgf, bgf_d), (rgp, rgp_d), (fl, fl_d)]:
            P.dma("sp", dst[:], src, writes=[cst])
        P.dma("pool", wg[:], win_d[:, 4096:4104].rearrange("(k p) c -> p k c", p=128), writes=[cst])

        if STOP == -1:
            ob = Buf("o")
            P.dma("sp", out_d[0:128, 0:128], ident[:], reads=[cst], writes=[ob])
            P.dma("sp", out_d[0:128, 128:256], mask01[:], reads=[cst], writes=[ob])
            P.dma("sp", out_d[0:4, 256:768], sel4[:].rearrange("k h s -> k (h s)"), reads=[cst], writes=[ob])
            P.emit("sp", lambda e: e.nop(), reads=[ob])
            P.run(es)
            return nc
        with ExitStack() as s0:
            cT = sb(s0, "cT", [128, 16], F32)
            sc = sb(s0, "sc", [128, 16], F32)
            modloc = sb(s0, "modloc", [1, ADA_W], F32)
            badas = sb(s0, "badas", [1, ADA_W], F32)
            btmf = sb(s0, "btmf", [1, 3072], F32)
            btmr = sb(s0, "btmr", [1, 3072], F32)
            mA = sb(s0, "mA", [96, 128], F32)
            mB = sb(s0, "mB", [48, 128], F32)
            b0 = Buf("s0")
            mlb = Buf("modloc")
            P.dma("sp", cT[:], cT_d, writes=[b0])
            P.dma("sp", badas[:], bada_d, writes=[b0])
            P.dma("sp", btmf[:], btm_d, writes=[b0])
            blo_t = sb(s0, "blo_t", [1, 3072], BF16)
            P.E("pool", "memset", writes=[cst], ap=bhl[:], constant=0.0)
            P.E("dve", "tensor_copy", reads=[b0, cst], writes=[cst], out=bhl[0:1, :], in_=btmf[:])
            P.E("dve", "tensor_copy", reads=[cst], writes=[b0], out=btmr[:], in_=bhl[0:1, :])
            P.E("dve", "tensor_tensor", reads=[b0], writes=[b0], out=btmr[:], in0=btmf[:], in1=btmr[:], op=ALU.subtract)
            P.E("dve", "tensor_copy", reads=[b0], writes=[b0], out=blo_t[:], in_=btmr[:])
            P.dma("sp", bhl[1:2, :], blo_t[:], reads=[b0], writes=[cst])
            P.E("act", "activation", reads=[b0], writes=[b0], out=sc[:], in_=cT[:], func=AF.Silu)
            if BIS == 1:
                ob = Buf("o")
                P.dma("sp", out_d[0:1, 0:1024], modloc[:, 0:1024], reads=[mlb, b0, cst], writes=[ob])
                P.emit("sp", lambda e: e.nop(), reads=[ob])
                P.run(es)
                return nc
            NU = ADA_W // 256
            for u in range(NU):
                s = ws_get()
                wv = ring[s][:, 0:8192].bitcast(F32).rearrange("p (k c) -> p k c", k=16)
                pb = u % 2
                for k in range(16):
                    P.E("pe", "matmul", reads=[b0, ringb[s]], writes=[psb[pb]], out=psum[pb][0:1, 0:256],
                        lhsT=sc[:, k:k + 1], rhs=wv[:, k, :], start=(k == 0), stop=(k == 15))
                P.E("dve", "tensor_tensor", reads=[psb[pb], b0], writes=[mlb], out=modloc[:, u * 256:(u + 1) * 256],
                    in0=psum[pb][0:1, 0:256], in1=badas[:, u * 256:(u + 1) * 256], op=ALU.add)
            if BIS == 2:
                ob = Buf("o")
                P.dma("sp", out_d[0:1, 0:1024], modloc[:, 0:1024], reads=[mlb, b0, cst], writes=[ob])
                P.emit("sp", lambda e: e.nop(), reads=[ob])
                P.run(es)
                return nc
            gb_in, gb_out = Buf("modin"), Buf("modout")
            P.dma("sp", modin_t.ap(), modloc[:], reads=[mlb], writes=[gb_in])
            P.cc(lambda e: e.collective_compute("AllGather", ALU.bypass, replica_groups=groups,
                                                ins=[modin_t.ap().opt()], outs=[modout_t.ap().opt()]),
                 cc_sems["mod"], "mod", reads=[gb_in], writes=[gb_out])
            mo144 = modout.rearrange("r (j p) -> (r j) p", p=128)
            P.dma("sp", mA[:], mo144[0:96, :], reads=[gb_out], writes=[b0])
            P.dma("sp", mB[:], mo144[96:144, :], reads=[gb_out], writes=[b0])
            if BIS == 3:
                ob = Buf("o")
                P.dma("sp", out_d[0:1, 0:1024], modloc[:, 0:1024], reads=[mlb, b0, cst], writes=[ob])
                P.emit("sp", lambda e: e.nop(), reads=[ob])
                P.run(es)
                return nc
            P.E("pe", "matmul", reads=[b0, cst], writes=[psb[2]], out=psum[2][:, 0:96], lhsT=mA[:], rhs=ident[0:96, 0:96], start=True, stop=True)
            P.E("pe", "matmul", reads=[b0, cst], writes=[psb[3]], out=psum[3][:, 0:48], lhsT=mB[:], rhs=ident[0:48, 0:48], start=True, stop=True)
            P.E("dve", "tensor_copy", reads=[psb[2]], writes=[modb], out=modT[:, 0:96], in_=psum[2][:, 0:96])
            P.E("dve", "tensor_copy", reads=[psb[3], modb], writes=[modb], out=modT[:, 96:144], in_=psum[3][:, 0:48])
            for s_ in range(3):
                c0 = (s_ * 3 + 1) * 16
                P.E("dve", "tensor_scalar", reads=[modb], writes=[modb], out=modT[:, c0:c0 + 16], in0=modT[:, c0:c0 + 16],
                    scalar1=1.0, scalar2=None, op0=ALU.add)
            P.E("dve", "tensor_scalar", reads=[cst], writes=[cst], out=bfm[:, 8:16], in0=bfm[:, 8:16], scalar1=KSC,
                scalar2=None, op0=ALU.mult)
            P.barrier()
        mo_flat = modout.rearrange("r w -> (r w)")
        if STOP == 0:
            ob = Buf("o")
            P.dma("sp", out_d[0:128, 0:144], modT[:], reads=[modb], writes=[ob])
            P.emit("sp", lambda e: e.nop(), reads=[ob])
            P.run(es)
            return nc

        def mod_col(sub, t, k):
            c = (sub * 3 + t) * 16 + k
            return modT[:, c:c + 1]

        def ln_stats(stk_small, z_ap, zb, eps, col, tagbuf):
            st = stk_small["bst"]
            for q in range(4):
                P.E("dve", "bn_stats", reads=[zb], writes=[tagbuf], out=st[:, q * 6:(q + 1) * 6],
                    in_=z_ap[:, q * 512:(q + 1) * 512])
            P.E("dve", "bn_aggr", reads=[tagbuf], writes=[tagbuf], out=small[:, col:col + 2], in_=st[:, :])
            P.E("act", "activation", reads=[tagbuf], writes=[tagbuf], out=small[:, col + 1:col + 2],
                in_=small[:, col + 1:col + 2], func=AF.Sqrt, bias=float(eps), scale=1.0)
            P.E("dve", "reciprocal", reads=[tagbuf], writes=[tagbuf], out=small[:, col + 1:col + 2],
                in_=small[:, col + 1:col + 2])

        def modulate_to_uT(stk_small, src_ap, srcb, sub, uT, uTb, tg, wk, wkb, psbank):
            sb_ = mstb
            ln_stats(stk_small, src_ap, srcb, EPS, 8, sb_)
            P.E("dve", "tensor_scalar", reads=[srcb, sb_], writes=[wkb], out=wk[:], in0=src_ap, scalar1=small[:, 8:9],
                scalar2=small[:, 9:10], op0=ALU.subtract, op1=ALU.mult)
            for k4 in range(4):
                pb = psbank[k4 % 2]
                for kk in range(4):
                    k = k4 * 4 + kk
                    P.E("pe", "matmul", reads=[wkb, cst], writes=[psb[pb]], out=psum[pb][:, kk * 128:(kk + 1) * 128],
                        lhsT=wk[:, k * 128:(k + 1) * 128], rhs=ident[:], start=True, stop=True)
                for kk in range(4):
                    k = k4 * 4 + kk
                    P.E("act", "activation", reads=[psb[pb], modb], writes=[uTb], out=uT[:, k, tg * 128:(tg + 1) * 128],
                        in_=psum[pb][:, kk * 128:(kk + 1) * 128], func=AF.Identity, bias=mod_col(sub, 0, k),
                        scale=mod_col(sub, 1, k))

        def load_bcast(stack, sub, gate_scale):
            gbc = sb(stack, "gbc", [128, D], F32)
            lgb = sb(stack, "lgb", [128, D], F32)
            lbb = sb(stack, "lbb", [128, D], F32)
            bb = Buf("bcast")
            g0 = (sub * 3 + 2) * D
            P.dma("sp", gbc[:], mo_flat[g0:g0 + D].partition_broadcast(128), writes=[bb])
            P.dma("sp", lgb[:], lng_d[sub, :].partition_broadcast(128), writes=[bb])
            P.dma("sp", lbb[:], lnb_d[sub, :].partition_broadcast(128), writes=[bb])
            P.E("dve", "tensor_scalar", reads=[bb], writes=[bb], out=gbc[:], in0=gbc[:], scalar1=float(gate_scale),
                scalar2=None, op0=ALU.mult)
            return gbc, lgb, lbb, bb

        def post_ln(stk_small, y_ap, yb, xres_ap, xresb, bc):
            gbc, lgb, lbb, bb = bc
            P.E("pool", "tensor_tensor", reads=[yb, bb], writes=[yb], out=y_ap, in0=y_ap, in1=gbc[:], op=ALU.mult)
            P.E("pool", "tensor_tensor", reads=[yb, xresb], writes=[yb], out=y_ap, in0=y_ap, in1=xres_ap, op=ALU.add)
            sb_ = pstb
            ln_stats(stk_small, y_ap, yb, EPSZ, 10, sb_)
            P.E("dve", "tensor_scalar", reads=[yb, sb_], writes=[yb], out=y_ap, in0=y_ap, scalar1=small[:, 10:11],
                scalar2=small[:, 11:12], op0=ALU.subtract, op1=ALU.mult)
            P.E("pool", "tensor_tensor", reads=[yb, bb], writes=[yb], out=y_ap, in0=y_ap, in1=lgb[:], op=ALU.mult)
            P.E("pool", "tensor_tensor", reads=[yb, bb], writes=[yb], out=y_ap, in0=y_ap, in1=lbb[:], op=ALU.add)

        def ffn_core(l, uT, uTb, hT, hTb, yacc, yaccb, sg, sgb):
            for u in range(22):
                s = ws_get()
                wv = ring[s][:, :].rearrange("p (k c) -> p k c", k=16)
                for fc in range(2):
                    f = u * 2 + fc
                    gp, vp = (f % 2), 2 + (f % 2)
                    for k in range(16):
                        P.E("pe", "matmul", reads=[ringb[s], uTb], writes=[psb[gp]], out=psum[gp][:, :],
                            lhsT=wv[:, k, fc * 128:(fc + 1) * 128], rhs=uT[:, k, :], start=(k == 0), stop=(k == 15))
                    for k in range(16):
                        P.E("pe", "matmul", reads=[ringb[s], uTb], writes=[psb[vp]], out=psum[vp][:, :],
                            lhsT=wv[:, k, 256 + fc * 128:256 + (fc + 1) * 128], rhs=uT[:, k, :], start=(k == 0),
                            stop=(k == 15))
                    P.E("act", "activation", reads=[psb[gp]], writes=[sgb[f % 2]], out=sg[f % 2][:], in_=psum[gp][:, :],
                        func=AF.Silu)
                    P.E("dve", "tensor_tensor", reads=[sgb[f % 2], psb[vp]], writes=[hTb[f]], out=hT[:, f, :],
                        in0=sg[f % 2][:], in1=psum[vp][:, :], op=ALU.mult)
            ngrp = 6
            for g in range(ngrp):
                nun = 2 if g < 5 else 1
                slots = [ws_get(look=NSLOT - 2) for _ in range(nun)]
                for tg in range(4):
                    base = (tg % 2) * 4
                    nf = nun * 4
                    for fi in range(nf):
                        s = slots[fi // 4]
                        f = g * 8 + fi
                        wv = ring[s][:, :].rearrange("p (j d) -> p j d", j=4)
                        for dc in range(4):
                            P.E("pe", "matmul", reads=[ringb[s], hTb[f]], writes=[psb[base + dc]],
                                out=psum[base + dc][:, :], lhsT=hT[:, f, tg * 128:(tg + 1) * 128],
                                rhs=wv[:, fi % 4, dc * 512:(dc + 1) * 512], start=(fi == 0), stop=(fi == nf - 1))
                    for dc in range(4):
                        dst = yacc[tg][:, dc * 512:(dc + 1) * 512]
                        if g == 0:
                            P.E("act", "activation", reads=[psb[base + dc]], writes=[yaccb[tg]], out=dst,
                                in_=psum[base + dc][:, :], func=AF.Copy)
                        else:
                            P.E("dve", "tensor_tensor", reads=[psb[base + dc], yaccb[tg]], writes=[yaccb[tg]], out=dst,
                                in0=psum[base + dc][:, :], in1=dst, op=ALU.add)

        dbg = {}
        with ExitStack() as s1:
            uT = sb(s1, "uT", [128, 16, 512], BF16)
            uTb = Buf("uT")
            hT = sb(s1, "hT", [128, 44, 512], BF16)
            hTb = [Buf(f"hT{f}") for f in range(44)]
            yacc = [sb(s1, f"yacc{i}", [128, D], F32) for i in range(4)]
            yaccb = [Buf(f"yacc{i}") for i in range(4)]
            xt = [sb(s1, f"xt{i}", [128, D], F32) for i in range(2)]
            xtb = [Buf(f"xt{i}") for i in range(2)]
            sg = [sb(s1, f"sg{i}", [128, 512], F32) for i in range(2)]
            sgb = [Buf(f"sg{i}") for i in range(2)]
            stg = [sb(s1, f"stg{i}", [128, 512], F32) for i in range(4)]
            stgb = [Buf(f"stg{i}") for i in range(4)]
            gtmp, gtb = sg, sgb
            stk = {"bst": sb(s1, "bst", [128, 24], F32)}
            bc = load_bcast(s1, 0, FFW / ALPHA)
            nstg = [0]

            def stage(dt=F32):
                i = nstg[0] % 4
                nstg[0] += 1
                ap = stg[i][:, :] if dt == F32 else stg[i][:, 0:256].bitcast(BF16)
                return ap, stgb[i]

            def mod_tile(t):
                for tg in range(4):
                    r0 = t * 512 + tg * 128
                    xi = (t * 4 + tg) % 2
                    P.dma("sp", xt[xi][:], x_d[r0:r0 + 128, :], writes=[xtb[xi]])
                    modulate_to_uT(stk, xt[xi][:], xtb[xi], 0, uT, uTb, tg, yacc[tg], yaccb[tg], (4, 5))

            for t in range(NTILE if BIS < 10 else 0):
                mod_tile(t)
                ffn_core(0, uT, uTb, hT, hTb, yacc, yaccb, sg, sgb)
                for tg in range(4):
                    r0 = t * 512 + tg * 128
                    xi = tg % 2
                    P.dma("sp", xt[xi][:], x_d[r0:r0 + 128, :], writes=[xtb[xi]])
                    post_ln(stk, yacc[tg][:], yaccb[tg], xt[xi][:], xtb[xi], bc)
                    P.dma("sp", x1_s[r0:r0 + 128, :], yacc[tg][:], reads=[yaccb[tg]])
                    if STOP == 1:
                        P.dma("sp", out_d[r0:r0 + 128, :], yacc[tg][:], reads=[yaccb[tg]])
                    modulate_to_uT(stk, yacc[tg][:], yaccb[tg], 1, uT, uTb, tg, xt[xi], xtb[xi], (4, 5))
                tok = slice(t * 512, (t + 1) * 512)
                for gi_, (bt, r0) in enumerate([(bgi, 0), (bgf, 4)]):
                    pb = 6 + gi_
                    for k in range(16):
                        P.E("pe", "matmul", reads=[uTb, cst], writes=[psb[pb]], out=psum[pb][0:4, :],
                            lhsT=wg[:, k, r0:r0 + 4], rhs=uT[:, k, :], start=(k == 0), stop=(k == 15))
                    ap, bf_ = stage()
                    P.E("act", "activation", reads=[psb[pb], cst], writes=[bf_], out=ap[0:4, :], in_=psum[pb][0:4, :],
                        func=AF.Identity, bias=bt[:, 0:1], scale=1.0)
                    P.dma("sp", gat_s[r0:r0 + 4, tok], ap[0:4, :], reads=[bf_])
                for wi in range(12):
                    s = ws_get()
                    wv = ring[s][:, :].rearrange("p (k c) -> p k c", k=16)
                    fm = wi in (0, 1, 2, 3, 8, 9, 10, 11)
                    tm = wi in (2, 3, 4, 5, 6, 7)
                    if fm:
                        for cc in range(4):
                            pb = cc % 4
                            for k in range(16):
                                P.E("pe", "matmul", reads=[ringb[s], uTb], writes=[psb[pb]], out=psum[pb][:, :],
                                    lhsT=wv[:, k, cc * 128:(cc + 1) * 128], rhs=uT[:, k, :], start=(k == 0),
                                    stop=(k == 15))
                            if wi < 4:
                                ci = wi * 4 + cc
                                ap, bf_ = stage(BF16)
                                P.E("act", "activation", reads=[psb[pb], cst], writes=[bf_], out=ap, in_=psum[pb][:, :],
                                    func=AF.Identity, bias=bfm[:, ci:ci + 1], scale=(1.0 if wi < 2 else KSC))
                                dst = qT_s if wi < 2 else kT_s
                                r = (ci % 8) * 128
                                P.dma("sp", dst[r:r + 128, tok], ap, reads=[bf_])
                            elif wi < 10:
                                ci = 16 + (wi - 8) * 4 + cc
                                ap, bf_ = stage()
                                P.E("act", "activation", reads=[psb[pb], cst], writes=[bf_], out=ap, in_=psum[pb][:, :],
                                    func=AF.Identity, bias=bfm[:, ci:ci + 1], scale=1.0)
                                r = (ci - 16) * 128
                                P.dma("sp", xrg_s[r:r + 128, tok], ap, reads=[bf_])
                            else:
                                ci = 24 + (wi - 10) * 4 + cc
                                ap, bf_ = stage()
                                g1_, g1b = gtmp[0], gtb[0]
                                g2_, g2b = gtmp[1], gtb[1]
                                P.E("act", "activation", reads=[psb[pb], cst], writes=[bf_], out=ap, in_=psum[pb][:, :],
                                    func=AF.Identity, bias=bfm[:, ci:ci + 1], scale=1.0)
                                P.E("pool", "tensor_tensor", reads=[bf_], writes=[g1b], out=g1_[:], in0=ap, in1=ap,
                                    op=ALU.mult)
                                P.E("pool", "tensor_scalar", reads=[g1b], writes=[g1b], out=g1_[:], in0=g1_[:],
                                    scalar1=0.044715, scalar2=1.0, op0=ALU.mult, op1=ALU.add)
                                P.E("pool", "tensor_tensor", reads=[g1b, bf_], writes=[g1b], out=g1_[:], in0=g1_[:], in1=ap,
                                    op=ALU.mult)
                                P.E("act", "activation", reads=[g1b], writes=[g2b], out=g2_[:], in_=g1_[:], func=AF.Tanh,
                                    scale=0.7978845608028654)
                                P.E("dve", "scalar_tensor_tensor", reads=[g2b, bf_], writes=[bf_], out=ap, in0=g2_[:],
                                    scalar=1.0, in1=ap, op0=ALU.add, op1=ALU.mult)
                                r = (ci - 24) * 128
                                P.dma("sp", gg_s[r:r + 128, tok], ap, reads=[bf_])
                    if tm:
                        c0 = (wi - 2) * 512
                        for tg in range(4):
                            pb = 4 + (tg % 4)
                            for k in range(16):
                                P.E("pe", "matmul", reads=[ringb[s], uTb], writes=[psb[pb]], out=psum[pb][:, :],
                                    lhsT=uT[:, k, tg * 128:(tg + 1) * 128], rhs=wv[:, k, :], start=(k == 0), stop=False)
                            P.E("pe", "matmul", reads=[cst], writes=[psb[pb]], out=psum[pb][:, :], lhsT=ones_bf[:, :],
                                rhs=bhl[:, c0:c0 + 512], start=False, stop=True)
                            ap, bf_ = stage(BF16)
                            r0 = t * 512 + tg * 128
                            if wi < 4:
                                P.E("act", "activation", reads=[psb[pb]], writes=[bf_], out=ap, in_=psum[pb][:, :],
                                    func=AF.Copy, scale=KSC)
                                dst = ktm_s[r0:r0 + 128, (wi - 2) * 512:(wi - 1) * 512]
                            elif wi < 6:
                                P.E("act", "activation", reads=[psb[pb]], writes=[bf_], out=ap, in_=psum[pb][:, :],
                                    func=AF.Copy)
                                dst = vtm_s[r0:r0 + 128, (wi - 4) * 512:(wi - 3) * 512]
                            else:
                                P.E("act", "activation", reads=[psb[pb]], writes=[bf_], out=ap, in_=psum[pb][:, :],
                                    func=AF.Sigmoid)
                                dst = og_s[r0:r0 + 128, (wi - 6) * 512:(wi - 5) * 512]
                            P.dma("sp", dst, ap, reads=[bf_])
            P.barrier()


        if STOP == 1:
            P.emit("sp", lambda e: e.nop())
            P.run(es)
            return nc

        def mixer_stage():
            au_s = dscr("au_s", [2, DR, NT]).ap()
            with ExitStack() as s2:
                T1 = sb(s2, "T1", [4, NT], F32)
                T2 = sb(s2, "T2", [4, NT], F32)
                T3 = sb(s2, "T3", [4, NT], F32)
                T4 = sb(s2, "T4", [4, NT], F32)
                gb_ = Buf("gates")
                tmS = sb(s2, "tmS", [128, NCH * 4], F32)
                tmA = sb(s2, "tmA", [128, NCH * 4], F32)
                tmW = sb(s2, "tmW", [128, NCH * 4], F32)
                tmF = sb(s2, "tmF", [128, NCH * 4], F32)
                tmK = sb(s2, "tmK", [128, NCH * 4], F32)
                decb = sb(s2, "decb", [128, 4 * NCH], F32)
                tmb = Buf("tm")
                sm4 = sb(s2, "sm4", [4, 64], F32)
                sm4b = Buf("sm4")
                ngb = sb(s2, "ngb", [128, DM], F32)
                P.dma("sp", ngb[:], ng_d[0, :].partition_broadcast(128), writes=[cst])
                P.dma("sp", T2[:], gat_s[0:4, :], writes=[gb_])
                P.dma("sp", T1[:], gat_s[4:8, :], writes=[gb_])
                G = [gb_]
                P.E("act", "activation", reads=G, writes=G, out=T4[:], in_=T1[:], func=AF.Abs)
                P.E("act", "activation", reads=G, writes=G, out=T4[:], in_=T4[:], func=AF.Exp, scale=-1.0)
                P.E("act", "activation", reads=G, writes=G, out=T4[:], in_=T4[:], func=AF.Ln, bias=1.0, scale=1.0)
                P.E("dve", "tensor_scalar", reads=G, writes=G, out=T3[:], in0=T1[:], scalar1=-1.0, scalar2=0.0, op0=ALU.mult,
                    op1=ALU.max)
                P.E("dve", "tensor_tensor", reads=G, writes=G, out=T3[:], in0=T3[:], in1=T4[:], op=ALU.add)
                P.E("dve", "tensor_scalar", reads=G, writes=G, out=T1[:], in0=T3[:], scalar1=-1.0, scalar2=None, op0=ALU.mult)
                P.E("dve", "memset", reads=G, writes=G, ap=T4[:], constant=1.0)
                P.E("dve", "tensor_tensor_scan", reads=G, writes=G, out=T3[:], data0=T4[:], data1=T1[:], initial=0.0,
                    op0=ALU.mult, op1=ALU.add)
                P.E("dve", "tensor_tensor", reads=G, writes=G, out=T2[:], in0=T2[:], in1=T3[:], op=ALU.subtract)
                P.E("dve", "tensor_reduce", reads=G, writes=G + [sm4b], out=sm4[:, 0:1], in_=T2[:], axis=AX.X, op=ALU.max)
                P.E("dve", "tensor_scalar", reads=G + [sm4b], writes=G, out=T4[:], in0=T2[:], scalar1=sm4[:, 0:1], scalar2=None,
                    op0=ALU.subtract)
                P.E("act", "activation", reads=G, writes=G, out=T4[:], in_=T4[:], func=AF.Exp)

                def to_tokmaj(src, dst):
                    for c in range(NCH):
                        P.E("pe", "matmul", reads=G + [cst], writes=[psb[0]], out=psum[0][:, c * 4:(c + 1) * 4],
                            lhsT=src[0:4, c * 128:(c + 1) * 128], rhs=ident[0:4, 0:4], start=True, stop=True)
                    P.E("dve", "tensor_copy", reads=[psb[0]], writes=[tmb], out=dst[:], in_=psum[0][:, 0:NCH * 4])

                to_tokmaj(T4, tmS)
                to_tokmaj(T2, tmA)

                kc = [sb(s2, f"kc{i}", [128, DM], BF16) for i in range(2)]
                vc = [sb(s2, f"vc{i}", [128, 4, 257], BF16) for i in range(2)]
                kcb = [Buf(f"kc{i}") for i in range(2)]
                vcb = [Buf(f"vc{i}") for i in range(2)]
                kw = [sb(s2, f"kw{i}", [128, 256], BF16) for i in range(2)]
                kwb = [Buf(f"kw{i}") for i in range(2)]
                for i in range(2):
                    P.E("pool", "memset", writes=[vcb[i]], ap=vc[i][:], constant=1.0)

                def load_kv(c):
                    i = c % 2
                    P.dma("sp", kc[i][:], ktm_s[c * 128:(c + 1) * 128, :], writes=[kcb[i]])
                    P.dma("sp", vc[i][:, :, 0:256], vtm_s[c * 128:(c + 1) * 128, :].rearrange("p (h e) -> p h e", h=4),
                          writes=[vcb[i]])

                n = 0
                for c in range(NCH):
                    load_kv(c)
                    i = c % 2
                    for h in range(4):
                        j = n % 2
                        n += 1
                        P.E("dve", "tensor_scalar", reads=[kcb[i], tmb], writes=[kwb[j]], out=kw[j][:],
                            in0=kc[i][:, h * 256:(h + 1) * 256], scalar1=tmS[:, c * 4 + h:c * 4 + h + 1], scalar2=None,
                            op0=ALU.mult)
                        for dc in range(2):
                            pb = h * 2 + dc
                            P.E("pe", "matmul", reads=[kwb[j], vcb[i]], writes=[psb[pb]], out=psum[pb][:, 0:257],
                                lhsT=kw[j][:, dc * 128:(dc + 1) * 128], rhs=vc[i][:, h, :], start=(c == 0),
                                stop=(c == NCH - 1))
                pay = sb(s2, "pay", [128, G1W], F32)
                payb = Buf("pay")
                for pb in range(8):
                    P.E("act" if pb % 2 else "dve", "activation" if pb % 2 else "tensor_copy", reads=[psb[pb]], writes=[payb],
                        out=pay[:, pb * 257:(pb + 1) * 257], in_=psum[pb][:, 0:257], **({"func": AF.Copy} if pb % 2 else {}))
                P.dma("sp", pay[:, 2056:2080].rearrange("p (b k) -> p b k", k=3),
                      xrg_s[:, NT - 3:NT].rearrange("(b p) k -> p b k", p=128), writes=[payb])
                g1i = [Buf(f"g1in{k}") for k in range(NG1)]
                g1o = [Buf(f"g1out{k}") for k in range(NG1)]
                ones4 = sb(s2, "ones4", [4, 128], F32)
                dg = sb(s2, "dg", [4, 8], F32)
                dgb = Buf("dg")
                P.E("dve", "memset", writes=[dgb], ap=ones4[:], constant=1.0)
                P.E("dve", "memset", reads=[payb], writes=[payb], ap=pay[:, 2080:G1W], constant=0.0)
                P.E("dve", "tensor_scalar", reads=G + [cst, dgb], writes=[dgb], out=dg[:, 0:4], in0=ident[0:4, 0:4],
                    scalar1=T3[:, NT - 1:NT], scalar2=None, op0=ALU.mult)
                P.E("dve", "tensor_scalar", reads=[sm4b, cst, dgb], writes=[dgb], out=dg[:, 4:8], in0=ident[0:4, 0:4],
                    scalar1=sm4[:, 0:1], scalar2=None, op0=ALU.mult)
                P.E("pe", "matmul", reads=[dgb], writes=[psb[0]], out=psum[0][:, 0:8], lhsT=ones4[:], rhs=dg[:], start=True,
                    stop=True)
                P.E("dve", "tensor_copy", reads=[psb[0], payb], writes=[payb], out=pay[:, 2080:2088], in_=psum[0][:, 0:8])
                for k in range(NG1):
                    P.dma("sp", g1in_t[k].ap(), pay[:, k * G1C:(k + 1) * G1C], reads=[payb], writes=[g1i[k]])

                    def mk(k):
                        return lambda e: e.collective_compute("AllGather", ALU.bypass, replica_groups=groups,
                                                              ins=[g1in_t[k].ap().opt()], outs=[g1out_t[k].ap().opt()])
                    if BIS == 13:
                        for r_ in range(4):
                            P.dma("sp", g1out_t[k].ap()[r_ * 128:(r_ + 1) * 128, :], g1in_t[k].ap(), reads=[g1i[k]],
                                  writes=[g1o[k]])
                    else:
                        P.cc(mk(k), cc_sems["g1"], "g1", reads=[g1i[k]], writes=[g1o[k]])

                Cst = sb(s2, "Cst", [128, 2056], F32)
                Cbf = sb(s2, "Cbf", [128, 2056], BF16)
                Cstb, Cbfb = Buf("Cst"), Buf("Cbf")
                Mrep = sb(s2, "Mrep", [128, 16], F32)
                Mb = Buf("Mrep")
                sci = sb(s2, "sci", [128, 8], F32)
                scib = Buf("sci")
                hal = sb(s2, "hal", [128, 24], F32)
                halt = sb(s2, "halt", [128, 24], F32)
                halb, haltb = Buf("hal"), Buf("halt")
                P.E("dve", "memset", writes=[Cstb], ap=Cst[:], constant=0.0)
                P.E("dve", "memset", writes=[Mb], ap=Mrep[:, 0:4], constant=-10000.0)
                P.E("dve", "memset", writes=[halb], ap=hal[:], constant=0.0)
                for i in range(3):
                    for k in range(NG1):
                        P.dma("sp", pay[:, k * G1C:(k + 1) * G1C], g1out_t[k].ap()[i * 128:(i + 1) * 128, :], reads=[g1o[k]],
                              writes=[payb])
                    P.E("dve", "tensor_copy", reads=[payb], writes=[haltb], out=halt[:], in_=pay[:, 2056:2080])
                    P.E("dve", "tensor_copy", reads=[payb], writes=[scib], out=sci[:], in_=pay[:, 2080:2088])
                    P.E("dve", "scalar_tensor_tensor", reads=[haltb, halb, cst], writes=[halb], out=hal[:], in0=halt[:],
                        scalar=fl[:, 3 + i:4 + i], in1=hal[:], op0=ALU.mult, op1=ALU.add)
                    mi = fl[:, i:i + 1]
                    P.E("dve", "tensor_tensor", reads=[Mb, scib], writes=[Mb], out=Mrep[:, 4:8], in0=Mrep[:, 0:4], in1=sci[:, 4:8],
                        op=ALU.max)
                    P.E("dve", "tensor_tensor", reads=[Mb], writes=[Mb], out=Mrep[:, 8:12], in0=Mrep[:, 0:4], in1=Mrep[:, 4:8],
                        op=ALU.subtract)
                    P.E("dve", "tensor_tensor", reads=[Mb, scib], writes=[Mb], out=Mrep[:, 12:16], in0=sci[:, 4:8],
                        in1=Mrep[:, 4:8], op=ALU.subtract)
                    P.E("act", "activation", reads=[Mb], writes=[Mb], out=Mrep[:, 8:16], in_=Mrep[:, 8:16], func=AF.Exp)
                    P.E("dve", "tensor_scalar", reads=[Mb, cst], writes=[Mb], out=Mrep[:, 12:16], in0=Mrep[:, 12:16], scalar1=mi,
                        scalar2=None, op0=ALU.mult)
                    P.E("dve", "tensor_scalar", reads=[Mb], writes=[Mb], out=Mrep[:, 8:12], in0=Mrep[:, 8:12], scalar1=-1.0,
                        scalar2=None, op0=ALU.add)
                    P.E("dve", "tensor_scalar", reads=[Mb, cst], writes=[Mb], out=Mrep[:, 8:12], in0=Mrep[:, 8:12], scalar1=mi,
                        scalar2=None, op0=ALU.mult)
                    P.E("dve", "tensor_scalar", reads=[Mb], writes=[Mb], out=Mrep[:, 8:12], in0=Mrep[:, 8:12], scalar1=1.0,
                        scalar2=None, op0=ALU.add)
                    for h in range(4):
                        blk = slice(h * 514, (h + 1) * 514)
                        P.E("dve", "tensor_scalar", reads=[Cstb, Mb], writes=[Cstb], out=Cst[:, blk], in0=Cst[:, blk],
                            scalar1=Mrep[:, 8 + h:9 + h], scalar2=None, op0=ALU.mult)
                        P.E("dve", "scalar_tensor_tensor", reads=[Cstb, Mb, payb], writes=[Cstb], out=Cst[:, blk], in0=pay[:, blk],
                            scalar=Mrep[:, 12 + h:13 + h], in1=Cst[:, blk], op0=ALU.mult, op1=ALU.add)
                    P.E("dve", "tensor_tensor", reads=[Mb, scib], writes=[Mb], out=Mrep[:, 4:8], in0=Mrep[:, 4:8], in1=sci[:, 0:4],
                        op=ALU.add)
                    P.E("dve", "tensor_tensor", reads=[Mb], writes=[Mb], out=Mrep[:, 4:8], in0=Mrep[:, 4:8], in1=Mrep[:, 0:4],
                        op=ALU.subtract)
                    P.E("dve", "scalar_tensor_tensor", reads=[Mb, cst], writes=[Mb], out=Mrep[:, 0:4], in0=Mrep[:, 4:8], scalar=mi,
                        in1=Mrep[:, 0:4], op0=ALU.mult, op1=ALU.add)
                P.E("act", "activation", reads=[Cstb], writes=[Cbfb], out=Cbf[:], in_=Cst[:], func=AF.Copy)
                mscr = dscr("mscr", [1, 8]).ap()
                mscb = Buf("mscr")
                P.dma("sp", mscr[0:1, 0:4], Mrep[0:1, 0:4], reads=[Mb], writes=[mscb])
                P.dma("sp", sm4[0:4, 1:2], mscr[0:1, 0:4].rearrange("o f -> f o"), reads=[mscb], writes=[sm4b])
                P.E("dve", "tensor_tensor_scan", reads=G + [sm4b], writes=G, out=T1[:], data0=T2[:], data1=T2[:],
                    initial=sm4[:, 1:2], op0=ALU.max, op1=ALU.max)
                Gv = T1[:].rearrange("p (c t) -> p c t", t=128)
                T4v = T4[:].rearrange("p (c t) -> p c t", t=128)
                T2v = T2[:].rearrange("p (c t) -> p c t", t=128)
                gend = sm4[:, 8:8 + NCH]
                gprev = sm4[:, 32:32 + NCH]
                P.E("dve", "tensor_copy", reads=G, writes=[sm4b], out=gend, in_=Gv[:, :, 127])
                P.E("dve", "tensor_copy", reads=[sm4b], writes=[sm4b], out=sm4[:, 32:33], in_=sm4[:, 1:2])
                if NCH > 1:
                    P.E("dve", "tensor_copy", reads=[sm4b], writes=[sm4b], out=sm4[:, 33:32 + NCH], in_=sm4[:, 8:8 + NCH - 1])
                P.E("dve", "tensor_tensor", reads=G + [sm4b], writes=G, out=T4v, in0=gprev.unsqueeze(2).to_broadcast([4, NCH, 128]),
                    in1=Gv, op=ALU.subtract)
                P.E("act", "activation", reads=G, writes=G, out=T4[:], in_=T4[:], func=AF.Exp)
                to_tokmaj(T4, tmW)
                dec4 = sb(s2, "dec4", [4, NCH], F32)
                P.E("dve", "tensor_copy", reads=G, writes=[sm4b], out=dec4[:], in_=T4v[:, :, 127])
                for h in range(4):
                    P.E("pe", "matmul", reads=[sm4b, cst], writes=[psb[1]], out=psum[1][:, h * NCH:(h + 1) * NCH],
                        lhsT=sel4[:, h, :], rhs=dec4[:], start=True, stop=True)
                P.E("dve", "tensor_copy", reads=[psb[1]], writes=[tmb], out=decb[:], in_=psum[1][:, 0:4 * NCH])
                P.E("dve", "tensor_tensor", reads=G, writes=G, out=T4[:], in0=T3[:], in1=T1[:], op=ALU.add)
                P.E("act", "activation", reads=G, writes=G, out=T4[:], in_=T4[:], func=AF.Exp, scale=-1.0)
                to_tokmaj(T4, tmF)
                P.E("dve", "tensor_tensor", reads=G + [sm4b], writes=G, out=T4v, in0=T2v,
                    in1=gend.unsqueeze(2).to_broadcast([4, NCH, 128]), op=ALU.subtract)
                P.E("act", "activation", reads=G, writes=G, out=T4[:], in_=T4[:], func=AF.Exp)
                to_tokmaj(T4, tmK)
                if BIS in (11, 13):
                    P.barrier()
                    return

                with ExitStack() as sr:
                    xp = sb(s2, "xp", [128, NT + 4], F32)
                    xc = sb(s2, "xc", [128, NT], F32)
                    xcb16 = sb(s2, "xcb16", [128, NT], BF16)
                    Rr = sb(s2, "Rr", [128, NT], F32)
                    Ii = sb(s2, "Ii", [128, NT], F32)
                    Aa = sb(s2, "Aa", [128, NT], F32)
                    Hh = sb(s2, "Hh", [128, NT], F32)
                    wab = sb(s2, "wab", [128, 8, 128], BF16)
                    wxb = sb(s2, "wxb", [128, 8, 128], BF16)
                    cpl = sb(s2, "cpl", [128, 16], F32)
                    g2p = sb(s2, "g2p", [128, 16], F32)
                    xpb, xcb_, xc16b, Rb, Ib, Ab, Hb = [Buf(x) for x in ["xp", "xc", "xc16", "R", "I", "A", "H"]]
                    wb, cplb, g2pb = Buf("rgw"), Buf("cpl"), Buf("g2p")
                    aub = [Buf(f"au{i}") for i in range(16)]
                    P.dma("pool", wab[:], rgwa_d.rearrange("n d e -> d n e"), writes=[wb])
                    P.dma("pool", wxb[:], rgwx_d.rearrange("n d e -> d n e"), writes=[wb])
                    P.E("act", "activation", reads=[cst], writes=[cplb], out=cpl[:, 0:8], in_=rgp[:, 56:64], func=AF.Exp, scale=-1.0)
                    P.E("act", "activation", reads=[cplb], writes=[cplb], out=cpl[:, 0:8], in_=cpl[:, 0:8], func=AF.Ln, bias=1.0,
                        scale=1.0)
                    P.E("dve", "tensor_scalar", reads=[cplb], writes=[cplb], out=cpl[:, 8:16], in0=cpl[:, 0:8], scalar1=-16.0,
                        scalar2=None, op0=ALU.mult)
                    P.E("dve", "tensor_scalar", reads=[cplb], writes=[cplb], out=cpl[:, 0:8], in0=cpl[:, 0:8], scalar1=-8.0,
                        scalar2=None, op0=ALU.mult)
                    for n_ in range(8):
                        P.dma("sp", xp[:, 3:3 + NT], xrg_s[n_ * 128:(n_ + 1) * 128, :], writes=[xpb])
                        P.E("dve", "tensor_copy", reads=[halb], writes=[xpb], out=xp[:, 0:3], in_=hal[:, n_ * 3:(n_ + 1) * 3])
                        P.E("dve", "tensor_scalar", reads=[xpb, cst], writes=[xcb_], out=xc[:], in0=xp[:, 3:3 + NT],
                            scalar1=rgp[:, n_ * 4 + 3:n_ * 4 + 4], scalar2=rgp[:, 32 + n_:33 + n_], op0=ALU.mult, op1=ALU.add)
                        for k in range(1, 4):
                            P.E("dve", "scalar_tensor_tensor", reads=[xpb, xcb_, cst], writes=[xcb_], out=xc[:],
                                in0=xp[:, 3 - k:3 - k + NT], scalar=rgp[:, n_ * 4 + 3 - k:n_ * 4 + 4 - k], in1=xc[:],
                                op0=ALU.mult, op1=ALU.add)
                        P.E("act", "activation", reads=[xcb_], writes=[xc16b], out=xcb16[:], in_=xc[:], func=AF.Copy)
                        for t in range(NT // 512):
                            ts_ = slice(t * 512, (t + 1) * 512)
                            for wi_, (wt, dst, db, bcol) in enumerate([(wab, Rr, Rb, 40 + n_), (wxb, Ii, Ib, 48 + n_)]):
                                pb = 2 + (t * 2 + wi_) % 4
                                P.E("pe", "matmul", reads=[wb, xc16b], writes=[psb[pb]], out=psum[pb][:, :], lhsT=wt[:, n_, :],
                                    rhs=xcb16[:, ts_], start=True, stop=True)
                                P.E("act", "activation", reads=[psb[pb], cst], writes=[db], out=dst[:, ts_], in_=psum[pb][:, :],
                                    func=AF.Sigmoid, bias=rgp[:, bcol:bcol + 1], scale=1.0)
                        P.E("act", "activation", reads=[Rb, cplb], writes=[Ab], out=Aa[:], in_=Rr[:], func=AF.Exp,
                            scale=cpl[:, n_:n_ + 1])
                        P.E("dve", "tensor_reduce", reads=[Rb], writes=[g2pb], out=g2p[:, n_:n_ + 1], in_=Rr[:], axis=AX.X,
                            op=ALU.add)
                        P.E("act", "activation", reads=[Rb, cplb], writes=[Rb], out=Rr[:], in_=Rr[:], func=AF.Exp,
                            scale=cpl[:, 8 + n_:9 + n_])
                        P.E("act", "activation", reads=[Rb], writes=[Rb], out=Rr[:], in_=Rr[:], func=AF.Sqrt, bias=1.0, scale=-1.0)
                        P.E("pool", "tensor_tensor", reads=[Ib, xcb_], writes=[Ib], out=Ii[:], in0=Ii[:], in1=xc[:], op=ALU.mult)
                        P.E("pool", "tensor_tensor", reads=[Ib, Rb], writes=[Ib], out=Ii[:], in0=Ii[:], in1=Rr[:], op=ALU.mult)
                        P.E("dve", "tensor_tensor_scan", reads=[Ab, Ib], writes=[Hb], out=Hh[:], data0=Aa[:], data1=Ii[:],
                            initial=0.0, op0=ALU.mult, op1=ALU.add)
                        P.E("dve", "tensor_copy", reads=[Hb], writes=[g2pb], out=g2p[:, 8 + n_:9 + n_], in_=Hh[:, NT - 1:NT])
                        P.dma("sp", au_s[0, n_ * 128:(n_ + 1) * 128, :], Aa[:], reads=[Ab], writes=[aub[n_]])
                        P.dma("sp", au_s[1, n_ * 128:(n_ + 1) * 128, :], Ii[:], reads=[Ib], writes=[aub[8 + n_]])
                    P.E("dve", "tensor_tensor", reads=[g2pb, cplb], writes=[g2pb], out=g2p[:, 0:8], in0=g2p[:, 0:8], in1=cpl[:, 0:8],
                        op=ALU.mult)
                    P.E("act", "activation", reads=[g2pb], writes=[g2pb], out=g2p[:, 0:8], in_=g2p[:, 0:8], func=AF.Exp)
                    g2i, g2o = Buf("g2in"), Buf("g2out")
                    P.dma("sp", g2in_t.ap(), g2p[:], reads=[g2pb], writes=[g2i])
                    P.cc(lambda e: e.collective_compute("AllGather", ALU.bypass, replica_groups=groups,
                                                        ins=[g2in_t.ap().opt()], outs=[g2out_t.ap().opt()]),
                         cc_sems["g2"], "g2", reads=[g2i], writes=[g2o])
                    hin = sb(s2, "hin", [128, 16], F32)
                    hinb = Buf("hin")
                    g2t = sb(s2, "g2t", [128, 16], F32)
                    g2tb = Buf("g2t")
                    P.E("dve", "memset", writes=[hinb], ap=hin[:], constant=0.0)
                    for i in range(3):
                        P.dma("sp", g2t[:], g2out_t.ap()[i * 128:(i + 1) * 128, :], reads=[g2o], writes=[g2tb])
                        P.E("dve", "tensor_tensor", reads=[hinb, g2tb], writes=[hinb], out=hin[:, 8:16], in0=hin[:, 0:8],
                            in1=g2t[:, 0:8], op=ALU.mult)
                        P.E("dve", "tensor_tensor", reads=[hinb, g2tb], writes=[hinb], out=hin[:, 8:16], in0=hin[:, 8:16],
                            in1=g2t[:, 8:16], op=ALU.add)
                        P.E("dve", "tensor_tensor", reads=[hinb], writes=[hinb], out=hin[:, 8:16], in0=hin[:, 8:16], in1=hin[:, 0:8],
                            op=ALU.subtract)
                        P.E("dve", "scalar_tensor_tensor", reads=[hinb, cst], writes=[hinb], out=hin[:, 0:8], in0=hin[:, 8:16],
                            scalar=fl[:, i:i + 1], in1=hin[:, 0:8], op0=ALU.mult, op1=ALU.add)
                    for n_ in range(8):
                        P.dma("sp", Aa[:], au_s[0, n_ * 128:(n_ + 1) * 128, :], reads=[aub[n_]], writes=[Ab])
                        P.dma("sp", Ii[:], au_s[1, n_ * 128:(n_ + 1) * 128, :], reads=[aub[8 + n_]], writes=[Ib])
                        P.dma("sp", xc[:], gg_s[n_ * 128:(n_ + 1) * 128, :], writes=[xcb_])
                        P.E("dve", "tensor_tensor_scan", reads=[Ab, Ib, hinb], writes=[Hb], out=Hh[:], data0=Aa[:], data1=Ii[:],
                            initial=hin[:, n_:n_ + 1], op0=ALU.mult, op1=ALU.add)
                        P.E("dve", "scalar_tensor_tensor", reads=[Hb, xcb_], writes=[xc16b], out=xcb16[:], in0=Hh[:], scalar=0.5,
                            in1=xc[:], op0=ALU.mult, op1=ALU.mult)
                        P.dma("sp", hc_s[DM + n_ * 128:DM + (n_ + 1) * 128, :], xcb16[:], reads=[xc16b])

                if BIS == 12:
                    P.barrier()
                    return
                with ExitStack() as sm:
                    qc = [sb(s2, f"qc{i}", [128, 8, 128], BF16) for i in range(2)]
                    ktc = [sb(s2, f"ktc{i}", [128, 8, 128], BF16) for i in range(2)]
                    oc = [sb(s2, f"oc{i}", [128, DM], BF16) for i in range(2)]
                    qcb = [Buf(f"qc{i}") for i in range(2)]
                    ktcb = [Buf(f"ktc{i}") for i in range(2)]
                    ocb = [Buf(f"oc{i}") for i in range(2)]
                    dng = sb(s2, "dng", [128, 128], F32)
                    Ee = sb(s2, "Ee", [128, 128], F32)
                    sTw = sb(s2, "sTw", [128, 128], BF16)
                    t1 = sb(s2, "t1", [128, 257], F32)
                    Nn = sb(s2, "Nn", [128, 257], F32)
                    hm = sb(s2, "hm", [128, 256], F32)
                    hg = sb(s2, "hg", [128, 256], BF16)
                    hT_ = sb(s2, "hT_", [128, 2, 128], BF16)
                    st6 = sb(s2, "st6", [128, 6], F32)
                    sc8 = sb(s2, "sc8", [128, 8], F32)
                    dngb, Eb, sTwb, t1b, Nnb, hmb, hgb, hTb_, scb = [Buf(x) for x in
                                                                    ["dng", "E", "sTw", "t1", "Nn", "hm", "hg", "hT_", "sc8"]]
                    for c in range(NCH):
                        i = c % 2
                        cs = slice(c * 128, (c + 1) * 128)
                        load_kv(c)
                        P.dma("sp", qc[i][:], qT_s[:, cs].rearrange("(f p) t -> p f t", p=128), writes=[qcb[i]])
                        P.dma("sp", ktc[i][:], kT_s[:, cs].rearrange("(f p) t -> p f t", p=128), writes=[ktcb[i]])
                        P.dma("sp", oc[i][:], og_s[cs, :], writes=[ocb[i]])
                        for h in range(4):
                            col = c * 4 + h
                            hb = slice(h * 514, (h + 1) * 514)
                            for dc in range(2):
                                P.E("pe", "matmul", reads=[ktcb[i], qcb[i]], writes=[psb[0]], out=psum[0][:, 0:128],
                                    lhsT=ktc[i][:, h * 2 + dc, :], rhs=qc[i][:, h * 2 + dc, :], start=(dc == 0), stop=(dc == 1))
                            P.E("pe", "matmul", reads=G + [cst], writes=[psb[1]], out=psum[1][:, 0:128], lhsT=sel4[:, h, :],
                                rhs=T1[0:4, cs], start=True, stop=True)
                            P.E("dve", "tensor_scalar", reads=[psb[1], tmb], writes=[dngb], out=dng[:], in0=psum[1][:, 0:128],
                                scalar1=tmA[:, col:col + 1], scalar2=None, op0=ALU.subtract)
                            P.E("pool", "tensor_scalar", reads=[dngb], writes=[dngb], out=dng[:], in0=dng[:], scalar1=0.0,
                                scalar2=None, op0=ALU.max)
                            P.E("act", "activation", reads=[dngb], writes=[Eb], out=Ee[:], in_=dng[:], func=AF.Exp, scale=-1.0)
                            P.E("pool", "tensor_tensor", reads=[Eb, cst], writes=[Eb], out=Ee[:], in0=Ee[:], in1=mask01[:],
                                op=ALU.mult)
                            P.E("dve", "tensor_tensor", reads=[psb[0], Eb], writes=[sTwb], out=sTw[:], in0=psum[0][:, 0:128],
                                in1=Ee[:], op=ALU.mult)
                            P.E("pe", "matmul", reads=[sTwb, vcb[i]], writes=[psb[2]], out=psum[2][:, 0:257], lhsT=sTw[:],
                                rhs=vc[i][:, h, :], start=True, stop=True)
                            for dc in range(2):
                                P.E("pe", "matmul", reads=[qcb[i], Cbfb], writes=[psb[3]], out=psum[3][:, 0:257],
                                    lhsT=qc[i][:, h * 2 + dc, :], rhs=Cbf[:, (h * 2 + dc) * 257:(h * 2 + dc + 1) * 257],
                                    start=(dc == 0), stop=(dc == 1))
                            P.E("act", "activation", reads=[psb[3], tmb], writes=[t1b], out=t1[:], in_=psum[3][:, 0:257],
                                func=AF.Copy, scale=tmW[:, col:col + 1])
                            P.E("dve", "tensor_tensor", reads=[t1b, psb[2]], writes=[Nnb], out=Nn[:], in0=psum[2][:, 0:257],
                                in1=t1[:], op=ALU.add)
                            P.E("act", "activation", reads=[Nnb], writes=[scb], out=sc8[:, 0:1], in_=Nn[:, 256:257], func=AF.Abs)
                            P.E("dve", "tensor_scalar", reads=[scb, tmb], writes=[scb], out=sc8[:, 0:1], in0=sc8[:, 0:1],
                                scalar1=tmF[:, col:col + 1], scalar2=None, op0=ALU.max)
                            P.E("dve", "reciprocal", reads=[scb], writes=[scb], out=sc8[:, 1:2], in_=sc8[:, 0:1])
                            P.E("act", "activation", reads=[Nnb, scb], writes=[hmb], out=hm[:], in_=Nn[:, 0:256], func=AF.Copy,
                                scale=sc8[:, 1:2])
                            P.E("dve", "bn_stats", reads=[hmb], writes=[scb], out=st6[:], in_=hm[:])
                            P.E("dve", "bn_aggr", reads=[scb], writes=[scb], out=sc8[:, 2:4], in_=st6[:])
                            P.E("act", "activation", reads=[scb], writes=[scb], out=sc8[:, 3:4], in_=sc8[:, 3:4], func=AF.Sqrt,
                                bias=EPS, scale=1.0)
                            P.E("dve", "reciprocal", reads=[scb], writes=[scb], out=sc8[:, 3:4], in_=sc8[:, 3:4])
                            P.E("dve", "tensor_scalar", reads=[hmb, scb], writes=[hmb], out=hm[:], in0=hm[:], scalar1=sc8[:, 2:3],
                                scalar2=sc8[:, 3:4], op0=ALU.subtract, op1=ALU.mult)
                            P.E("pool", "tensor_tensor", reads=[hmb, cst], writes=[hmb], out=hm[:], in0=hm[:],
                                in1=ngb[:, h * 256:(h + 1) * 256], op=ALU.mult)
                            P.E("pool", "tensor_tensor", reads=[hmb, ocb[i]], writes=[hgb], out=hg[:], in0=hm[:],
                                in1=oc[i][:, h * 256:(h + 1) * 256], op=ALU.mult)
                            for dc in range(2):
                                P.E("pe", "matmul", reads=[hgb, cst], writes=[psb[4]], out=psum[4][:, dc * 128:(dc + 1) * 128],
                                    lhsT=hg[:, dc * 128:(dc + 1) * 128], rhs=identb[:], start=True, stop=True)
                            P.E("act", "activation", reads=[psb[4]], writes=[hTb_], out=hT_[:].rearrange("p a t -> p (a t)"),
                                in_=psum[4][:, 0:256], func=AF.Copy)
                            P.dma("sp", hc_s[h * 256:(h + 1) * 256, cs].rearrange("(a p) t -> p a t", p=128), hT_[:],
                                  reads=[hTb_])
                            j = n % 2
                            n += 1
                            P.E("dve", "tensor_scalar", reads=[kcb[i], tmb], writes=[kwb[j]], out=kw[j][:],
                                in0=kc[i][:, h * 256:(h + 1) * 256], scalar1=tmK[:, col:col + 1], scalar2=None, op0=ALU.mult)
                            for dc in range(2):
                                pb = 5 + dc
                                P.E("pe", "matmul", reads=[kwb[j], vcb[i]], writes=[psb[pb]], out=psum[pb][:, 0:257],
                                    lhsT=kw[j][:, dc * 128:(dc + 1) * 128], rhs=vc[i][:, h, :], start=True, stop=True)
                                cb_ = slice((h * 2 + dc) * 257, (h * 2 + dc + 1) * 257)
                                P.E("dve", "scalar_tensor_tensor", reads=[Cstb, tmb, psb[pb]], writes=[Cstb], out=Cst[:, cb_],
                                    in0=Cst[:, cb_], scalar=decb[:, h * NCH + c:h * NCH + c + 1], in1=psum[pb][:, 0:257],
                                    op0=ALU.mult, op1=ALU.add)
                            P.E("act", "activation", reads=[Cstb], writes=[Cbfb], out=Cbf[:, hb], in_=Cst[:, hb], func=AF.Copy)
                P.barrier()

        if MIX:
            mixer_stage()
        else:
            with ExitStack() as s2:
                zt = sb(s2, "zt", [128, NT], BF16)
                zb = Buf("zt")
                P.E("dve", "memset", writes=[zb], ap=zt[:], constant=0.0)
                for k in range(16):
                    P.dma("sp", hc_s[k * 128:(k + 1) * 128, :], zt[:], reads=[zb])
                P.barrier()

        with ExitStack() as s3:
            wo = sb(s3, "wo", [128, 16, D], BF16)
            wob = Buf("wo")
            for u in range(4 if BIS < 10 else 0):
                for hf in range(2):
                    P.dma("pool", wo[:, u * 4:(u + 1) * 4, hf * 1024:(hf + 1) * 1024],
                          wout_d[u * 512:(u + 1) * 512, hf * 1024:(hf + 1) * 1024].rearrange("(j p) d -> p j d", p=128),
                          writes=[wob])
            hcT = [sb(s3, "hcT0", [128, 16, 512], BF16)] * 2
            hcTb = [Buf("hcT0")] * 2
            u3T = sb(s3, "u3T", [128, 16, 512], BF16)
            u3Tb = Buf("u3T")
            yt = [sb(s3, f"yt{i}", [128, D], F32) for i in range(2)]
            ytb = [Buf(f"yt{i}") for i in range(2)]
            xt = [sb(s3, "xt30", [128, D], F32)] * 2
            xtb = [Buf("xt30")] * 2
            stk = {"bst": sb(s3, "bst3", [128, 24], F32)}
            bc = load_bcast(s3, 1, 1.0 / ALPHA)
            for t in range(NTILE if BIS < 10 else 0):
                tok = slice(t * 512, (t + 1) * 512)
                hi = t % 2
                P.dma("sp", hcT[hi][:], hc_s[:, tok].rearrange("(k p) t -> p k t", p=128), writes=[hcTb[hi]])
                for tg in range(4):
                    r0 = t * 512 + tg * 128
                    base = (tg % 2) * 4
                    yi = tg % 2
                    P.dma("sp", xt[yi][:], x1_s[r0:r0 + 128, :], writes=[xtb[yi]])
                    for f in range(16):
                        for dc in range(4):
                            P.E("pe", "matmul", reads=[wob, hcTb[hi]], writes=[psb[base + dc]], out=psum[base + dc][:, :],
                                lhsT=hcT[hi][:, f, tg * 128:(tg + 1) * 128], rhs=wo[:, f, dc * 512:(dc + 1) * 512],
                                start=(f == 0), stop=(f == 15))
                    for dc in range(4):
                        P.E("act", "activation", reads=[psb[base + dc]], writes=[ytb[yi]],
                            out=yt[yi][:, dc * 512:(dc + 1) * 512], in_=psum[base + dc][:, :], func=AF.Copy)
                    post_ln(stk, yt[yi][:], ytb[yi], xt[yi][:], xtb[yi], bc)
                    P.dma("sp", x2_s[r0:r0 + 128, :], yt[yi][:], reads=[ytb[yi]])
                    modulate_to_uT(stk, yt[yi][:], ytb[yi], 2, u3T, u3Tb, tg, xt[yi], xtb[yi], (2, 6))
                P.dma("sp", u3_s[:, tok].rearrange("(k p) t -> p k t", p=128), u3T[:], reads=[u3Tb])
            P.barrier()

        with ExitStack() as s4:
            uT = [sb(s4, "uT4", [128, 16, 512], BF16)]
            uTb = [Buf("uT4")]
            hT = sb(s4, "hT4", [128, 44, 512], BF16)
            hTb = [Buf(f"hT4{f}") for f in range(44)]
            yacc = [sb(s4, f"yacc4{i}", [128, D], F32) for i in range(4)]
            yaccb = [Buf(f"yacc4{i}") for i in range(4)]
            xt = [sb(s4, f"xt4{i}", [128, D], F32) for i in range(2)]
            xtb = [Buf(f"xt4{i}") for i in range(2)]
            sg = [sb(s4, f"sg4{i}", [128, 512], F32) for i in range(2)]
            sgb = [Buf(f"sg4{i}") for i in range(2)]
            stk = {"bst": sb(s4, "bst4", [128, 24], F32)}
            bc = load_bcast(s4, 2, FFW / ALPHA)
            outbs = []
            for t in range(NTILE if BIS < 10 else 0):
                ui = 0
                P.dma("sp", uT[0][:], u3_s[:, t * 512:(t + 1) * 512].rearrange("(k p) t -> p k t", p=128),
                      writes=[uTb[0]])
                ffn_core(1, uT[ui], uTb[ui], hT, hTb, yacc, yaccb, sg, sgb)
                for tg in range(4):
                    r0 = t * 512 + tg * 128
                    xi = tg % 2
                    P.dma("sp", xt[xi][:], x2_s[r0:r0 + 128, :], writes=[xtb[xi]])
                    post_ln(stk, yacc[tg][:], yaccb[tg], xt[xi][:], xtb[xi], bc)
                    ob = Buf("out")
                    outbs.append(ob)
                    P.dma("sp", out_d[r0:r0 + 128, :], yacc[tg][:], reads=[yaccb[tg]], writes=[ob])
            P.emit("sp", lambda e: e.nop(), reads=outbs)
        P.run(es)
    return nc


def prep_inputs(inp, NT):
    f32 = np.float32
    g = lambda k: np.asarray(inp[k], dtype=f32)
    x, c = g("x"), g("c")
    b_in = g("b_in")[0]
    w_ada, b_ada = g("w_ada")[0], g("b_ada")[0]
    shared = {
        "ffn1_w13": np.ascontiguousarray(g("ffn1_w13")[0]), "ffn1_w2": np.ascontiguousarray(g("ffn1_w2")[0]),
        "ffn2_w13": np.ascontiguousarray(g("ffn2_w13")[0]), "ffn2_w2": np.ascontiguousarray(g("ffn2_w2")[0]),
        "w_in": np.ascontiguousarray(g("w_in")[0]), "w_out": np.ascontiguousarray(g("w_out")[0]),
        "b_fm": np.ascontiguousarray(np.concatenate([b_in[0:2048], b_in[4104:6152]]).reshape(32, 128).T),
        "b_tm": np.ascontiguousarray(b_in[1024:4096][None, :]),
        "b_gi": np.ascontiguousarray(b_in[4096:4100][:, None]),
        "b_gf": np.ascontiguousarray(b_in[4100:4104][:, None]),
        "norm_g": np.ascontiguousarray(g("mlstm_norm_g")[0].reshape(1, 1024)),
        "rg_wa": np.ascontiguousarray(g("rg_wa")[0]), "rg_wx": np.ascontiguousarray(g("rg_wx")[0]),
        "ln_g": np.ascontiguousarray(g("ln_g")[0]), "ln_b": np.ascontiguousarray(g("ln_b")[0]),
    }
    rgp = np.zeros((128, 64), f32)
    cw = g("rg_conv_w")[0]
    rgp[:, 0:32] = cw.reshape(4, 8, 128).transpose(2, 1, 0).reshape(128, 32)
    for i, k in enumerate(["rg_conv_b", "rg_ba", "rg_bx", "rg_lambda"]):
        rgp[:, 32 + 8 * i:40 + 8 * i] = g(k)[0].reshape(8, 128).T
    shared["rgp"] = rgp
    maps = []
    for core in range(NCORE):
        b, j = core // GRP, core % GRP
        m = dict(shared)
        m["x"] = np.ascontiguousarray(x[b, j * NT:(j + 1) * NT, :])
        m["cT"] = np.ascontiguousarray(c[b].reshape(16, 128).T)
        m["w_ada"] = np.ascontiguousarray(w_ada[:, j * ADA_W:(j + 1) * ADA_W])
        m["b_ada"] = np.ascontiguousarray(b_ada[j * ADA_W:(j + 1) * ADA_W][None, :])
        fl = np.zeros((128, 8), f32)
        for i in range(3):
            fl[:, i] = 1.0 if i < j else 0.0
            fl[:, 3 + i] = 1.0 if i == j - 1 else 0.0
        fl[:, 6] = 1.0 if j > 0 else 0.0
        m["flags"] = fl
        maps.append(m)
    return maps


_NC_CACHE = {}


def run_cores(inp, NT, MIX=True, trace=False, STOP=9):
    key = (NT, MIX, STOP)
    if key not in _NC_CACHE:
        _NC_CACHE[key] = build(NT, MIX, STOP)
    nc = _NC_CACHE[key]
    maps = prep_inputs(inp, NT)
    if STOP <= 0 or BIS >= 10:
        for m in maps:
            for k in ("ffn1_w13", "ffn1_w2", "ffn2_w13", "ffn2_w2", "w_out"):
                m[k] = np.zeros((128, 128), np.float32)
    res = run_bass_kernel_spmd(nc, maps, core_ids=list(range(NCORE)), **({"trace": True} if trace else {}))
    S = NT * GRP
    out = np.zeros((2, S, D), np.float32)
    for core in range(NCORE):
        b, j = core // GRP, core % GRP
        out[b, j * NT:(j + 1) * NT, :] = res.results[core]["out"]
    return out, res


def kernel(**inputs):
    out, _ = run_cores(inputs, 2048, True)
    return out
```
